# Optimizing a Trainium2 kernel written in Bass

```python
import jax, jax.numpy as jnp
from jax import lax
import numpy as np

D_MODEL = 1024
BATCH = 2
SEQ = 8192
DEPTH = 4

N_MEM = 256
EPS = 1e-6
N_BRANCH = 4
BRANCH_W = D_MODEL // 2
GM_W = BRANCH_W
GM_GROUPS = 4
GM_CHUNK = 128
LRU_W = BRANCH_W
LRU_BLOCKS = 8
LRU_CONV = 4
LRU_C = 8.0
HEAD_DIM = 64
SWA_HEADS = BRANCH_W // HEAD_DIM
SWA_KV = 2
WINDOW = 128
ROPE_THETA = 500000.0
ROT_DIM = HEAD_DIM // 4
XA_HEADS = 4
XA_DIM = BRANCH_W // XA_HEADS
D_FF = 4 * D_MODEL

IN_SIZES = [2 * GM_W, 2 * LRU_W, SWA_HEADS * HEAD_DIM, SWA_KV * HEAD_DIM, SWA_KV * HEAD_DIM,
            XA_HEADS * XA_DIM, N_BRANCH * D_MODEL]
D_IN = sum(IN_SIZES)
IN_SPLITS = [int(v) for v in np.cumsum(IN_SIZES)[:-1]]

kernel_name = "griffin_hybrid_gmlp_rglru_swa_xattn"


def rmsnorm(x, g):
    xf = x.astype(jnp.float32)
    y = xf * lax.rsqrt(jnp.mean(xf * xf, axis=-1, keepdims=True) + EPS)
    return (y * g.astype(jnp.float32)).astype(x.dtype)


def rope_tables(seq):
    pos = jnp.arange(seq, dtype=jnp.float32)
    inv = ROPE_THETA ** (-jnp.arange(0, ROT_DIM, 2, dtype=jnp.float32) / ROT_DIM)
    ang = pos[:, None] * inv[None, :]
    return jnp.cos(ang), jnp.sin(ang)


def partial_rope(x, cos, sin):
    xf = x.astype(jnp.float32)
    half = ROT_DIM // 2
    x1, x2, xp = xf[..., :half], xf[..., half:ROT_DIM], xf[..., ROT_DIM:]
    c = cos[None, :, None, :]
    s = sin[None, :, None, :]
    out = jnp.concatenate([x1 * c - x2 * s, x2 * c + x1 * s, xp], axis=-1)
    return out.astype(x.dtype)


def gmlp_branch(z, v_gain, ws, bs):
    B, S, _ = z.shape
    z = jax.nn.gelu(z)
    u, v = jnp.split(z, 2, axis=-1)
    v = rmsnorm(v, v_gain)
    nc = S // GM_CHUNK
    v = v.reshape(B, nc, GM_CHUNK, GM_GROUPS, GM_W // GM_GROUPS)
    causal = jnp.tril(jnp.ones((GM_CHUNK, GM_CHUNK), dtype=bool))
    w = jnp.where(causal[None], ws, jnp.zeros_like(ws))
    s = jnp.einsum('gts,bnsgc->bntgc', w, v) + bs.T[:, :, None]
    return u * s.reshape(B, S, GM_W)


def rglru_branch(z, conv_w, conv_b, wr, br, wi, bi, lam):
    B, S, _ = z.shape
    xb, gb = jnp.split(z, 2, axis=-1)
    xp = jnp.pad(xb, ((0, 0), (LRU_CONV - 1, 0), (0, 0)))
    xc = conv_b + xp[:, 0:S] * conv_w[0]
    for k in range(1, LRU_CONV):
        xc = xc + xp[:, k:k + S] * conv_w[k]
    xh = xc.reshape(B, S, LRU_BLOCKS, LRU_W // LRU_BLOCKS)
    r_gate = jax.nn.sigmoid(jnp.einsum('bshi,hij->bshj', xh, wr).reshape(B, S, LRU_W) + br)
    i_gate = jax.nn.sigmoid(jnp.einsum('bshi,hij->bshj', xh, wi).reshape(B, S, LRU_W) + bi)
    log_a = -LRU_C * r_gate.astype(jnp.float32) * jax.nn.softplus(-lam.astype(jnp.float32))
    a = jnp.exp(log_a)
    mult = jnp.sqrt(-jnp.expm1(2.0 * log_a))
    b_in = xc.astype(jnp.float32) * i_gate.astype(jnp.float32) * mult

    def combine(lhs, rhs):
        a1, b1 = lhs
        a2, b2 = rhs
        return a1 * a2, a2 * b1 + b2

    _, h = lax.associative_scan(combine, (a, b_in), axis=1)
    return jax.nn.gelu(gb) * h.astype(gb.dtype)


def swa_branch(q, k, v, q_gain, k_gain, sinks, cos, sin):
    B, S, _ = q.shape
    G = SWA_HEADS // SWA_KV
    nb = S // WINDOW
    q = partial_rope(rmsnorm(q.reshape(B, S, SWA_HEADS, HEAD_DIM), q_gain), cos, sin)
    k = partial_rope(rmsnorm(k.reshape(B, S, SWA_KV, HEAD_DIM), k_gain), cos, sin)
    v = v.reshape(B, S, SWA_KV, HEAD_DIM)
    qb = q.reshape(B, nb, WINDOW, SWA_KV, G, HEAD_DIM)
    kb = k.reshape(B, nb, WINDOW, SWA_KV, HEAD_DIM)
    vb = v.reshape(B, nb, WINDOW, SWA_KV, HEAD_DIM)
    pad = ((0, 0), (1, 0), (0, 0), (0, 0), (0, 0))
    kk = jnp.concatenate([jnp.pad(kb, pad)[:, :-1], kb], axis=2)
    vv = jnp.concatenate([jnp.pad(vb, pad)[:, :-1], vb], axis=2)
    scores = jnp.einsum('bnikgd,bnjkd->bnkgij', qb, kk).astype(jnp.float32) * (HEAD_DIM ** -0.5)
    qi = jnp.arange(WINDOW)[:, None]
    kj = jnp.arange(2 * WINDOW)[None, :]
    diff = qi + WINDOW - kj
    band = (diff >= 0) & (diff < WINDOW)
    first_ok = (jnp.arange(nb)[:, None, None] > 0) | (kj[None] >= WINDOW)
    valid = band[None] & first_ok
    scores = jnp.where(valid[None, :, None, None], scores, -jnp.inf)
    sink = sinks.astype(jnp.float32).reshape(SWA_KV, G)[None, None, :, :, None, None]
    m = jnp.maximum(jnp.max(scores, axis=-1, keepdims=True), sink)
    p = jnp.exp(scores - m)
    p = p / (jnp.sum(p, axis=-1, keepdims=True) + jnp.exp(sink - m))
    o = jnp.einsum('bnkgij,bnjkd->bnikgd', p.astype(vv.dtype), vv)
    return o.reshape(B, S, SWA_HEADS * HEAD_DIM)


def cross_branch(q, mem_n, w_kv, q_gain, k_gain):
    B, S, _ = q.shape
    M = mem_n.shape[1]
    q = rmsnorm(q.reshape(B, S, XA_HEADS, XA_DIM), q_gain)
    k, v = jnp.split(mem_n @ w_kv, 2, axis=-1)
    k = rmsnorm(k.reshape(B, M, XA_HEADS, XA_DIM), k_gain)
    v = v.reshape(B, M, XA_HEADS, XA_DIM)
    s = jnp.einsum('bshd,bmhd->bhsm', q, k).astype(jnp.float32) * (XA_DIM ** -0.5)
    p = jax.nn.softmax(s, axis=-1)
    o = jnp.einsum('bhsm,bmhd->bshd', p.astype(v.dtype), v)
    return o.reshape(B, S, XA_HEADS * XA_DIM)


def setup_inputs(seed: int = 0) -> dict:
    key = jax.random.key(seed)
    ks = jax.random.split(key, 32)
    f32 = jnp.float32
    nrm = lambda k, shape, scale: jax.random.normal(k, shape, f32) * scale
    gain = lambda k, shape: 1.0 + 0.01 * jax.random.normal(k, shape, f32)
    L = DEPTH
    a_c = jax.random.uniform(ks[12], (L, LRU_W), f32, 0.9, 0.999)
    a0 = a_c ** (1.0 / LRU_C)
    lam = jnp.log(a0) - jnp.log1p(-a0)
    return {
        "x": nrm(ks[0], (BATCH, SEQ, D_MODEL), 1.0),
        "mem": nrm(ks[1], (BATCH, N_MEM, D_MODEL), 1.0),
        "norm_mix": gain(ks[2], (L, D_MODEL)),
        "norm_mem": gain(ks[3], (L, D_MODEL)),
        "norm_mlp": gain(ks[4], (L, D_MODEL)),
        "w_in": nrm(ks[5], (L, D_MODEL, D_IN), D_MODEL ** -0.5),
        "b_gate": nrm(ks[6], (L, N_BRANCH, D_MODEL), 0.01),
        "gm_v_gain": gain(ks[7], (L, GM_W)),
        "gm_ws": nrm(ks[8], (L, GM_GROUPS, GM_CHUNK, GM_CHUNK), GM_CHUNK ** -0.5),
        "gm_bs": gain(ks[9], (L, GM_GROUPS, GM_CHUNK)),
        "lru_conv_w": nrm(ks[10], (L, LRU_CONV, LRU_W), LRU_CONV ** -0.5),
        "lru_conv_b": nrm(ks[11], (L, LRU_W), 0.01),
        "lru_wr": nrm(ks[13], (L, LRU_BLOCKS, LRU_W // LRU_BLOCKS, LRU_W // LRU_BLOCKS), (LRU_W // LRU_BLOCKS) ** -0.5),
        "lru_br": nrm(ks[14], (L, LRU_W), 0.01),
        "lru_wi": nrm(ks[15], (L, LRU_BLOCKS, LRU_W // LRU_BLOCKS, LRU_W // LRU_BLOCKS), (LRU_W // LRU_BLOCKS) ** -0.5),
        "lru_bi": nrm(ks[16], (L, LRU_W), 0.01),
        "lru_lambda": lam,
        "swa_q_gain": gain(ks[17], (L, HEAD_DIM)),
        "swa_k_gain": gain(ks[18], (L, HEAD_DIM)),
        "swa_sinks": nrm(ks[19], (L, SWA_HEADS), 0.5),
        "w_mem_kv": nrm(ks[20], (L, D_MODEL, 2 * XA_HEADS * XA_DIM), D_MODEL ** -0.5),
        "xa_q_gain": gain(ks[21], (L, XA_DIM)),
        "xa_k_gain": gain(ks[22], (L, XA_DIM)),
        "w_branch": nrm(ks[23], (L, N_BRANCH, BRANCH_W, D_MODEL), BRANCH_W ** -0.5),
        "w_out": nrm(ks[24], (L, D_MODEL, D_MODEL), D_MODEL ** -0.5),
        "w_ff1": nrm(ks[25], (L, D_MODEL, D_FF), D_MODEL ** -0.5),
        "w_ff2": nrm(ks[26], (L, D_FF, D_MODEL), D_FF ** -0.5),
    }


def reference(x, mem, norm_mix, norm_mem, norm_mlp, w_in, b_gate, gm_v_gain, gm_ws, gm_bs,
              lru_conv_w, lru_conv_b, lru_wr, lru_br, lru_wi, lru_bi, lru_lambda,
              swa_q_gain, swa_k_gain, swa_sinks, w_mem_kv, xa_q_gain, xa_k_gain,
              w_branch, w_out, w_ff1, w_ff2):
    B, S, D = x.shape
    cos, sin = rope_tables(S)
    for l in range(DEPTH):
        h = rmsnorm(x, norm_mix[l])
        z = h @ w_in[l]
        z_gm, z_lru, q_s, k_s, v_s, q_x, z_gate = jnp.split(z, IN_SPLITS, axis=-1)
        mem_n = rmsnorm(mem, norm_mem[l])
        o_gm = gmlp_branch(z_gm, gm_v_gain[l], gm_ws[l], gm_bs[l])
        o_lru = rglru_branch(z_lru, lru_conv_w[l], lru_conv_b[l], lru_wr[l], lru_br[l],
                             lru_wi[l], lru_bi[l], lru_lambda[l])
        o_swa = swa_branch(q_s, k_s, v_s, swa_q_gain[l], swa_k_gain[l], swa_sinks[l], cos, sin)
        o_xa = cross_branch(q_x, mem_n, w_mem_kv[l], xa_q_gain[l], xa_k_gain[l])
        o = jnp.stack([o_gm, o_lru, o_swa, o_xa], axis=2)
        p = jnp.einsum('bsnc,ncd->bsnd', o, w_branch[l])
        g = jax.nn.sigmoid(z_gate.reshape(B, S, N_BRANCH, D) + b_gate[l])
        x = x + jnp.sum(g * p, axis=2) @ w_out[l]
        h = rmsnorm(x, norm_mlp[l])
        x = x + jnp.square(jax.nn.relu(h @ w_ff1[l])) @ w_ff2[l]
    return x
```

```python
import numpy as np
import concourse.bass as bass
import concourse.mybir as mybir
from concourse.bass_utils import run_bass_kernel_spmd

F32 = mybir.dt.float32
BF16 = mybir.dt.bfloat16
AF = mybir.ActivationFunctionType
ALU = mybir.AluOpType

NCORE = 8
T = 2048
TG = 512
NTG = 4
NBLK = 16
EPS = 1e-6
NSLOT = 10
TPL = 160

PP_GMIX, PP_GMLP, PP_GMEM = 0, 8, 16
PP_BG = 24
PP_BR = 56
PP_BI = 60
PP_CW = 64
PP_CB = 80
PP_LAM = 84
PP_SQG, PP_SKG, PP_XQG, PP_XKG = 88, 89, 90, 91
PP_SINK = 92
PP_VG = 100
PP_BS = 612
PP_N = 1124

CB_COS, CB_SIN = 0, 2048
CB_MASKN, CB_MASK0 = 4096, 4352
CB_ID, CB_ONES, CB_BONES, CB_ROT = 4608, 4736, 4864, 4992
CB_N = 5120


class Trk:
    def __init__(self, nc, engs, sems):
        self.nc = nc
        self.E = engs
        self.sem = sems
        self.cnt = {e: 0 for e in sems}
        self.seen = {e: {} for e in engs}
        self.st = {}

    def deps(self, reads, writes):
        need = {}
        for k in reads:
            s = self.st.get(k)
            if s and s[0]:
                n, v = s[0]
                need[n] = max(need.get(n, 0), v)
        for k in writes:
            s = self.st.get(k)
            if s:
                if s[0]:
                    n, v = s[0]
                    need[n] = max(need.get(n, 0), v)
                for n, v in s[1].items():
                    need[n] = max(need.get(n, 0), v)
        return need

    def wait(self, eng, need):
        for n, v in need.items():
            if eng == "pe" and n == "pe":
                continue
            if self.seen[eng].get(n, 0) >= v:
                continue
            self.E[eng].wait_ge(self.sem[n], v)
            self.seen[eng][n] = v

    def record(self, ev, reads, writes):
        n, v = ev
        for k in reads:
            s = self.st.setdefault(k, [None, {}])
            s[1][n] = max(s[1].get(n, 0), v)
        for k in writes:
            self.st[k] = [ev, {}]

    def op(self, eng, reads, writes, fn):
        self.wait(eng, self.deps(reads, writes))
        ins = fn(self.E[eng])
        self.cnt[eng] += 1
        ins.then_inc(self.sem[eng], 1)
        self.record((eng, self.cnt[eng]), reads, writes)

    def group(self, eng, reads, writes, fns):
        self.wait(eng, self.deps(reads, writes))
        ins = None
        for fn in fns:
            ins = fn(self.E[eng])
        self.cnt[eng] += 1
        ins.then_inc(self.sem[eng], 1)
        self.record((eng, self.cnt[eng]), reads, writes)

    def dma(self, q, semname, reads, writes, fn, inc=16):
        need = self.deps(reads, writes)
        if self.cnt[semname] > 0:
            need[semname] = max(need.get(semname, 0), self.cnt[semname])
        self.wait(q, need)
        ins = fn(self.E[q])
        self.cnt[semname] += inc
        ins.then_inc(self.sem[semname], inc)
        self.record((semname, self.cnt[semname]), reads, writes)


def build(n_layers=4, debug=None, order=None):
    nc = bass.Bass("TRN2", target_bir_lowering=False, dynamic_dma_scratch_size=4096)
    NT = n_layers * TPL
    d_x = nc.dram_tensor("xT", [128, 8, T], F32, kind="ExternalInput").ap()
    d_mem = nc.dram_tensor("memT", [128, 8, 256], F32, kind="ExternalInput").ap()
    d_w = nc.dram_tensor("wts", [NT, 128, 1024], F32, kind="ExternalInput").ap()
    d_pp = nc.dram_tensor("pp", [n_layers, 128, PP_N], F32, kind="ExternalInput").ap()
    d_cb = nc.dram_tensor("cb", [128, CB_N], F32, kind="ExternalInput").ap()
    d_cf = nc.dram_tensor("cf", [128, 16], F32, kind="ExternalInput").ap()
    d_y = nc.dram_tensor("yT", [128, 8, T], F32, kind="ExternalOutput").ap()
    d_xs = nc.dram_tensor("xs", [128, 5, T], F32, kind="Internal").ap()
    d_src1 = nc.dram_tensor("src1", [128, 16], F32, kind="Internal").ap()
    d_gat1 = nc.dram_tensor("gat1", [NCORE * 128, 16], F32, kind="Internal").ap()
    d_src2 = nc.dram_tensor("src2", [128, 264], F32, kind="Internal").ap()
    d_gat2 = nc.dram_tensor("gat2", [NCORE * 128, 264], F32, kind="Internal").ap()
    d_dbg = {}
    for nm in (debug or []):
        if nm.startswith("o"):
            d_dbg[nm] = nc.dram_tensor("dbg_" + nm, [128, 4, T], BF16, kind="ExternalOutput").ap()
        elif nm == "h":
            d_dbg[nm] = nc.dram_tensor("dbg_" + nm, [128, 8, T], BF16, kind="ExternalOutput").ap()
        else:
            d_dbg[nm] = nc.dram_tensor("dbg_" + nm, [128, 8, T], F32, kind="ExternalOutput").ap()
    debug = set(debug or [])

    import contextlib
    with contextlib.ExitStack() as es:
        def sb(name, shape, dt):
            return es.enter_context(nc.sbuf_tensor(name, shape, dt))

        xT = sb("xT_s", [128, 8, T], F32)
        hT = sb("hT_s", [128, 8, T], BF16)
        mg = sb("mg_s", [128, 8, T], BF16)
        oT = sb("oT_s", [128, 4, T], BF16)
        wsl = sb("wsl", [128, NSLOT, 1024], BF16)
        pp = sb("pp_s", [128, PP_N], F32)
        cb = sb("cb_s", [128, CB_N], BF16)
        cf = sb("cf_s", [128, 16], F32)
        Ft = [sb(f"F{i}", [128, 516], F32) for i in range(6)]
        Bt = [sb(f"B{i}", [128, 512], BF16) for i in range(4)]
        kT = sb("kT_s", [128, 128 + T], BF16)
        va = sb("va_s", [128, 17, 2, 66], BF16)
        qT = sb("qT_s", [128, 4, 512], BF16)
        Pt = sb("P_s", [128, 2048], BF16)
        ot = sb("ot_s", [128, 512], BF16)
        kTx = sb("kTx_s", [128, 4, 256], BF16)
        Vx = sb("Vx_s", [128, 2, 512], BF16)
        sm = sb("sm_s", [128, 256], F32)
        g1 = sb("g1_s", [128, 8, 16], F32)
        g2 = sb("g2_s", [128, 8, 8], F32)
        psf = es.enter_context(nc.psum_tensor("psf", [128, 7, 512], F32))
        psb = es.enter_context(nc.psum_tensor("psb", [128, 1024], BF16))

        SM_HB = 0
        SM_C1 = 40
        SM_ES = 44
        SM_HC = 52
        SM_PC = 56
        SM_HALO = 60
        SM_PUB1 = 76
        SM_HINIT = 92
        SM_TMP = 96
        SM_ZERO = 112
        SM_DEN = 120
        SM_RV = 128
        SM_PUB2 = 136

        semnames = ["pe", "act", "dve", "pool", "sp"] + [f"w{i}" for i in range(NSLOT)] + \
                   [f"d{i}" for i in range(8)] + [f"kc{i}" for i in range(3)] + [f"cc{i}" for i in range(2 * n_layers)]
        sems = {n: es.enter_context(nc.semaphore(n)) for n in semnames}
        block = es.enter_context(nc.Block())

        engs = {"pe": nc.tensor, "act": nc.scalar, "dve": nc.vector, "pool": nc.gpsimd, "sp": nc.sync}
        tk = Trk(nc, engs, sems)
        dctr = [0]

        def spdma(out, in_, reads, writes):
            i = dctr[0] % 8
            dctr[0] += 1
            tk.dma("sp", f"d{i}", reads, writes, lambda e: e.dma_start(out=out, in_=in_))

        recording = order is None
        worder = [] if recording else list(order)
        windex = {} if recording else {n: i for i, n in enumerate(worder)}
        wstate = {"issued": 0, "free": list(range(NSLOT)), "slot": {}}

        def w_dma(j, s_):
            tk.dma("pool", f"w{s_}", [], [("w", s_)],
                   lambda e, j=j, s_=s_: e.dma_start(out=wsl[:, s_, :], in_=d_w[j]))

        def w_issue():
            if recording:
                return
            while wstate["issued"] < len(worder) and wstate["free"]:
                j = wstate["issued"]
                s_ = wstate["free"].pop(0)
                wstate["slot"][j] = s_
                w_dma(j, s_)
                wstate["issued"] += 1

        def w_get(name):
            if recording:
                if name not in windex:
                    windex[name] = len(worder)
                    worder.append(name)
                    w_dma(windex[name], windex[name] % NSLOT)
                return windex[name] % NSLOT
            j = windex[name]
            assert j < wstate["issued"], f"weight tile {name} (#{j}) not issued: all slots pinned"
            return wstate["slot"][j]

        def w_rel(name):
            if recording:
                return
            wstate["free"].append(wstate["slot"][windex[name]])
            w_issue()

        pctr = [0]

        def ps1():
            i = pctr[0] % 7
            pctr[0] += 1
            return psf[:, i, :], [("ps", i)]

        def ps2():
            while pctr[0] % 7 not in (0, 2, 4):
                pctr[0] += 1
            i = pctr[0] % 7
            pctr[0] += 2
            return psf[:, i:i + 2, :], [("ps", i), ("ps", i + 1)]

        def act(out, in_, func, reads, writes, bias=None, scale=None):
            kw = {}
            if bias is not None:
                kw["bias"] = bias
            if scale is not None:
                kw["scale"] = scale
            tk.op("act", reads, writes, lambda e: e.activation(out=out, in_=in_, func=func, **kw))

        def tt(out, in0, in1, op, reads, writes):
            tk.op("dve", reads, writes, lambda e: e.tensor_tensor(out=out, in0=in0, in1=in1, op=op))

        def ts(out, in0, s1, s2, op0, op1, reads, writes):
            if op1 is None:
                tk.op("dve", reads, writes, lambda e: e.tensor_scalar(out=out, in0=in0, scalar1=s1, scalar2=None, op0=op0))
            else:
                tk.op("dve", reads, writes, lambda e: e.tensor_scalar(out=out, in0=in0, scalar1=s1, scalar2=s2, op0=op0, op1=op1))

        def stt(out, in0, scalar, in1, op0, op1, reads, writes):
            tk.op("dve", reads, writes, lambda e: e.scalar_tensor_tensor(out=out, in0=in0, scalar=scalar, in1=in1, op0=op0, op1=op1))

        def vcopy(out, in_, reads, writes):
            tk.op("dve", reads, writes, lambda e: e.tensor_copy(out=out, in_=in_))

        def mmg(out, pairs, reads, writes):
            n = len(pairs)
            fns = [(lambda e, a=a, b=b, i=i: e.matmul(out, lhsT=a, rhs=b, start=(i == 0), stop=(i == n - 1)))
                   for i, (a, b) in enumerate(pairs)]
            tk.group("pe", reads, writes, fns)

        def bc_free(ap2d, n_rep, inner):
            return bass.AP(ap2d.tensor, ap2d.offset, [list(ap2d.ap[0]), [0, n_rep], [1, inner]])

        def bc_last(ap2d, outer, n_rep):
            st = ap2d.ap[1][0]
            return bass.AP(ap2d.tensor, ap2d.offset, [list(ap2d.ap[0]), [st, outer], [0, n_rep]])

        def smc(off, n=1):
            return sm[:, off:off + n]

        def tgs(tg):
            return slice(tg * TG, (tg + 1) * TG)

        hkeys = lambda tg: [("h", c, tg) for c in range(8)]
        oT2 = xT[:, 6:8, :].rearrange("p c t -> p (c t)").bitcast(BF16).rearrange("p (c t) -> p c t", c=4)
        G2v = xT[:, 3:5, :].rearrange("p c t -> p (c t)").bitcast(BF16).rearrange("p (c t) -> p c t", c=4)
        GG = [("xg", 3), ("xg", 4)]
        OB = [oT[:], oT2]
        OG = [[], [("xg", 6), ("xg", 7)]]
        MT = [xT[:, 5, i * 512:(i + 1) * 512] for i in range(4)]
        MG = [("xg", 5)]

        def ok(ob, c, tg):
            return ("o", ob, c, tg)

        def interleave(A, B, rate):
            acc = 0.0
            a_done = b_done = False
            while not (a_done and b_done):
                if not a_done:
                    try:
                        next(A)
                    except StopIteration:
                        a_done = True
                if a_done:
                    acc = 1e9
                acc += rate
                while acc >= 1.0 and not b_done:
                    acc -= 1.0
                    try:
                        next(B)
                    except StopIteration:
                        b_done = True
        ident = cb[:, CB_ID:CB_ID + 128]
        ones = cb[:, CB_ONES:CB_ONES + 128]
        bones = cb[:, CB_BONES:CB_BONES + 128]
        rotm = cb[:, CB_ROT:CB_ROT + 128]
        CK = [("cb",)]

        for c in range(8):
            spdma(xT[:, c, :], d_x[:, c, :], [], [("x", c, tg) for tg in range(NTG)])
        spdma(cf[:], d_cf, [], [("cf",)])
        for i, (a, b) in enumerate([(0, 2048), (2048, 4096), (4096, 5120)]):
            tk.dma("pool", f"kc{i}", [], [("cbp", i)],
                   lambda e, a=a, b=b: e.dma_start(out=cb[:, a:b], in_=d_cb[:, a:b]))
        CK = [("cbp", 0), ("cbp", 1), ("cbp", 2)]
        w_issue()
        tk.op("dve", [], [("va", j) for j in range(6)] , lambda e: e.memset(va[:], 1.0))
        tk.op("dve", [], [("zero",)], lambda e: e.memset(sm[:, SM_ZERO:SM_ZERO + 8], 0.0))
        zero_bc = bass.AP(sm[:, SM_ZERO:SM_ZERO + 1].tensor, sm[:, SM_ZERO:SM_ZERO + 1].offset,
                          [list(sm[:, SM_ZERO:SM_ZERO + 1].ap[0]), [0, 512]])

        def rmsnorm_to_hT(gcol):
            for tg in range(NTG):
                norm_tg(gcol, tg)

        def norm_tg(gcol, tg):
            if True:
                bk, bkk = ps1()
                for c in range(8):
                    sq = Bt[c % 4]
                    act(sq[:], xT[:, c, tgs(tg)], AF.Square, [("x", c, tg)], [("B", c % 4)])
                    tk.op("pe", [("B", c % 4)] + CK, bkk,
                          lambda e, sq=sq, c=c: e.matmul(bk, lhsT=ones, rhs=sq[:], start=(c == 0), stop=(c == 7)))
                rs = Ft[5]
                act(rs[:, 0:512], bk, AF.Ln, bkk, [("F", 5)], bias=EPS, scale=1.0 / 1024)
                act(rs[:, 0:512], rs[:, 0:512], AF.Exp, [("F", 5)], [("F", 5)], scale=-0.5)
                for c in range(8):
                    stt(hT[:, c, tgs(tg)], xT[:, c, tgs(tg)], pp[:, gcol + c:gcol + c + 1], rs[:, 0:512],
                        ALU.mult, ALU.mult, [("x", c, tg), ("F", 5), ("pp",)], [("h", c, tg)])

        def normrope(bank, bankk, gcol, out_ap, out_keys, tok0):
            zf, sq, rs, kn, t1, t2 = Ft[0], Bt[0], Ft[1], Bt[1], Ft[2], Ft[3]
            act(zf[:, 0:512], bank, AF.Copy, bankk, [("F", 0)])
            act(sq[:], bank, AF.Square, bankk, [("B", 0)])
            b2, b2k = ps1()
            mmg(b2, [(bones, sq[:])], [("B", 0)] + CK, b2k)
            act(rs[:, 0:512], b2, AF.Ln, b2k, [("F", 1)], bias=EPS, scale=1.0 / 64)
            act(rs[:, 0:512], rs[:, 0:512], AF.Exp, [("F", 1)], [("F", 1)], scale=-0.5)
            stt(kn[:], zf[:, 0:512], pp[:, gcol:gcol + 1], rs[:, 0:512], ALU.mult, ALU.mult,
                [("F", 0), ("F", 1), ("pp",)], [("B", 1)])
            b3, b3k = ps1()
            mmg(b3, [(rotm, kn[:])], [("B", 1)] + CK, b3k)
            tt(t1[:, 0:512], kn[:], cb[:, CB_COS + tok0:CB_COS + tok0 + 512], ALU.mult, [("B", 1)] + CK, [("F", 2)])
            tt(t2[:, 0:512], b3, cb[:, CB_SIN + tok0:CB_SIN + tok0 + 512], ALU.mult, b3k + CK, [("F", 3)])
            tt(out_ap, t1[:, 0:512], t2[:, 0:512], ALU.add, [("F", 2), ("F", 3)], out_keys)

        def merge_gen(L, b, ob, first):
            oTb = OB[ob]
            for mp in range(4):
                nb = (L, "mb", b, mp)
                sbr = w_get(nb)
                for mm in range(2):
                    m = 2 * mp + mm
                    ng = (L, "mgt", b, m)
                    sg = w_get(ng)
                    for tg in range(NTG):
                        zg, zgk = ps1()
                        mmg(zg, [(wsl[:, sg, k * 128:(k + 1) * 128], hT[:, k, tgs(tg)]) for k in range(8)],
                            [("w", sg)] + hkeys(tg), zgk)
                        pb, pbk = ps1()
                        mmg(pb, [(wsl[:, sbr, k * 256 + mm * 128:k * 256 + mm * 128 + 128], oTb[:, k, tgs(tg)]) for k in range(4)],
                            [("w", sbr)] + [ok(ob, k, tg) for k in range(4)] + OG[ob], pbk)
                        fi = tg % 2
                        th = MT[fi]
                        act(th, zg, AF.Tanh, zgk + [("hb",)] + MG, [("MT", fi)],
                            bias=sm[:, SM_HB + b * 8 + m:SM_HB + b * 8 + m + 1], scale=0.5)
                        if first:
                            stt(mg[:, m, tgs(tg)], th, 1.0, pb, ALU.add, ALU.mult,
                                [("MT", fi)] + pbk + MG, [("mg", m, tg)])
                        else:
                            t2 = MT[2 + fi]
                            stt(t2, th, 1.0, pb, ALU.add, ALU.mult, [("MT", fi)] + pbk + MG, [("MT", 2 + fi)])
                            tt(mg[:, m, tgs(tg)], mg[:, m, tgs(tg)], t2, ALU.add,
                               [("MT", 2 + fi), ("mg", m, tg)] + MG, [("mg", m, tg)])
                        yield
                    w_rel(ng)
                w_rel(nb)

        ALLO = lambda ob: [ok(ob, c, tg) for c in range(4) for tg in range(NTG)]

        def swa_main(L, ob):
            oTb, og = OB[ob], OG[ob]
            nq = [(L, "q", g) for g in range(4)]
            sq_ = [w_get(n_) for n_ in nq]
            for tg in range(NTG):
                for g in range(4):
                    bk, bkk = ps1()
                    mmg(bk, [(wsl[:, sq_[g], k * 128:(k + 1) * 128], hT[:, k, tgs(tg)]) for k in range(8)],
                        [("w", sq_[g])] + hkeys(tg), bkk)
                    normrope(bk, bkk, PP_SQG, qT[:, g, :], [("qT", g)], tg * TG)
                    yield
                for bi in range(4):
                    n = tg * 4 + bi
                    mcol = CB_MASK0 if n == 0 else CB_MASKN
                    for kv in range(2):
                        pr = slice(kv * 64, (kv + 1) * 64)
                        sc, sck = ps2()
                        scf = sc.rearrange("p a b -> p (a b)")
                        fns = []
                        for g in range(4):
                            for cc in range(2):
                                kc0 = (n + cc) * 128
                                o_ = scf[:, (g * 2 + cc) * 128:(g * 2 + cc + 1) * 128]
                                fns.append(lambda e, o_=o_, kc0=kc0, g=g, pr=pr, bi=bi: e.matmul(
                                    o_, lhsT=kT[pr, kc0:kc0 + 128], rhs=qT[pr, g, bi * 128:(bi + 1) * 128],
                                    start=True, stop=True))
                        kkeys = [("kT", 0)] if n == 0 else []
                        kkeys += [("kT", 1 + tg)] + ([("kT", tg)] if (bi == 0 and tg > 0) else [])
                        tk.group("pe", kkeys + [("qT", g) for g in range(4)], sck, fns)
                        Ph = Pt[:, kv * 1024:(kv + 1) * 1024]
                        act(Ph, scf, AF.Exp, sck, [("P", kv)], scale=0.125)
                        tt(Ph.rearrange("p (g x) -> p g x", g=4), Ph.rearrange("p (g x) -> p g x", g=4),
                           bc_free(cb[:, mcol:mcol + 256], 4, 256), ALU.mult, [("P", kv)] + CK, [("P", kv)])
                        yield
                        pv, pvk = ps1()
                        for g in range(4):
                            prs = []
                            for cc in range(2):
                                prs.append((Pt[:, kv * 1024 + (g * 2 + cc) * 128:kv * 1024 + (g * 2 + cc + 1) * 128],
                                            va[:, n + cc, kv, 0:65]))
                            vkeys = [("va", 0)] if n == 0 else []
                            vkeys += [("va", 1 + tg)] + ([("va", tg)] if (bi == 0 and tg > 0) else [])
                            mmg(pv[:, g * 128:g * 128 + 65], prs, [("P", kv)] + vkeys, pvk)
                        pv3 = pv.rearrange("p (g x) -> p g x", g=4)
                        den = smc(SM_DEN + kv * 4, 4)
                        tt(den, pv3[:, :, 64], smc(SM_ES + kv * 4, 4), ALU.add, pvk + [("es",)], [("den", kv)])
                        tk.op("dve", [("den", kv)], [("den", kv)], lambda e, den=den: e.reciprocal(out=den, in_=den))
                        tt(ot[:, kv * 256:(kv + 1) * 256].rearrange("p (g d) -> p g d", g=4), pv3[:, :, 0:64],
                           bc_last(den, 4, 64), ALU.mult, pvk + [("den", kv)], [("ot", kv)])
                        yield
                    fns = [(lambda e, c=c: e.transpose(out=psb[:, c * 128:(c + 1) * 128], in_=ot[:, c * 128:(c + 1) * 128],
                                                        identity=ident)) for c in range(4)]
                    tk.group("pe", [("ot", 0), ("ot", 1)] + CK, [("psb",)], fns)
                    act(oTb[:, :, n * 128:(n + 1) * 128], psb[:, 0:512].rearrange("p (c t) -> p c t", c=4), AF.Copy,
                        [("psb",)] + og, [ok(ob, c, tg) for c in range(4)])
                    yield
            for n_ in nq:
                w_rel(n_)

        def gmlp_main(L, ob):
            oTb, og = OB[ob], OG[ob]
            for c in range(4):
                nm = (L, "gu", c)
                s_ = w_get(nm)
                for tg in range(NTG):
                    bk, bkk = ps1()
                    mmg(bk, [(wsl[:, s_, k * 128:(k + 1) * 128], hT[:, k, tgs(tg)]) for k in range(8)],
                        [("w", s_)] + hkeys(tg), bkk)
                    act(oTb[:, c, tgs(tg)], bk, AF.Gelu_apprx_tanh, bkk + og, [ok(ob, c, tg)])
                    yield
                w_rel(nm)
            nws = (L, "gws")
            sws = w_get(nws)
            wsv = wsl[:, sws, 0:512].rearrange("p (g t) -> p g t", g=4)
            tt(wsv, wsv, bc_free(cb[:, CB_MASKN + 128:CB_MASKN + 256], 4, 128), ALU.mult, [("w", sws)] + CK, [("w", sws)])
            ngv = [(L, "gv", j) for j in range(4)]
            sgv = [w_get(n_) for n_ in ngv]
            for blk in range(NBLK):
                tg = blk // 4
                bk, bkk = ps1()
                for j in range(4):
                    mmg(bk[:, j * 128:(j + 1) * 128],
                        [(hT[:, k, blk * 128:(blk + 1) * 128], wsl[:, sgv[j], k * 128:(k + 1) * 128]) for k in range(8)],
                        [("w", sgv[j])] + hkeys(tg), bkk)
                vg, sqv = Ft[0], Ft[1]
                act(vg[:, 0:512], bk, AF.Gelu_apprx_tanh, bkk, [("F", 0)])
                act(sqv[:, 0:512], vg[:, 0:512], AF.Square, [("F", 0)], [("F", 1)])
                rv = smc(SM_RV)
                tk.op("dve", [("F", 1)], [("rv",)],
                      lambda e, rv=rv, sqv=sqv: e.tensor_reduce(out=rv, in_=sqv[:, 0:512], axis=mybir.AxisListType.X, op=ALU.add))
                act(rv, rv, AF.Ln, [("rv",)], [("rv",)], bias=EPS, scale=1.0 / 512)
                act(rv, rv, AF.Exp, [("rv",)], [("rv",)], scale=-0.5)
                stt(Bt[0][:], vg[:, 0:512], rv, pp[:, PP_VG:PP_VG + 512], ALU.mult, ALU.mult,
                    [("F", 0), ("rv",), ("pp",)], [("B", 0)])
                yield
                b2, b2k = ps1()
                fns = [(lambda e, g=g, b2=b2: e.matmul(b2[:, g * 128:(g + 1) * 128], lhsT=Bt[0][:, g * 128:(g + 1) * 128],
                                                rhs=wsl[:, sws, g * 128:(g + 1) * 128], start=True, stop=True)) for g in range(4)]
                tk.group("pe", [("B", 0), ("w", sws)], b2k, fns)
                t3 = Ft[2]
                tt(t3[:, 0:512], b2, pp[:, PP_BS:PP_BS + 512], ALU.add, b2k + [("pp",)], [("F", 2)])
                ov = oTb[:, :, blk * 128:(blk + 1) * 128]
                tt(ov, ov, t3[:, 0:512].rearrange("p (g t) -> p g t", g=4), ALU.mult,
                   [("F", 2)] + [ok(ob, c, tg) for c in range(4)] + og, [ok(ob, c, tg) for c in range(4)])
                yield
            for n_ in ngv:
                w_rel(n_)
            w_rel(nws)

        def xattn_main(L, ob):
            oTb, og = OB[ob], OG[ob]
            memn = Pt[:].rearrange("p (c m) -> p c m", c=8)
            PK = [("P", 0), ("P", 1)]
            bk, bkk = ps1()
            for c in range(8):
                tmp = Ft[c % 2]
                spdma(tmp[:, 0:256], d_mem[:, c, :], [], [("F", c % 2)])
                act(Bt[c % 2][:, 0:256], tmp[:, 0:256], AF.Square, [("F", c % 2)], [("B", c % 2)])
                tk.op("pe", [("B", c % 2)] + CK, bkk,
                      lambda e, c=c, bk=bk: e.matmul(bk[:, 0:256], lhsT=ones, rhs=Bt[c % 2][:, 0:256], start=(c == 0), stop=(c == 7)))
            yield
            rs = Ft[2]
            act(rs[:, 0:256], bk[:, 0:256], AF.Ln, bkk, [("F", 2)], bias=EPS, scale=1.0 / 1024)
            act(rs[:, 0:256], rs[:, 0:256], AF.Exp, [("F", 2)], [("F", 2)], scale=-0.5)
            for c in range(8):
                tmp = Ft[c % 2]
                spdma(tmp[:, 0:256], d_mem[:, c, :], [], [("F", c % 2)])
                stt(memn[:, c, :], tmp[:, 0:256], pp[:, PP_GMEM + c:PP_GMEM + c + 1], rs[:, 0:256], ALU.mult, ALU.mult,
                    [("F", c % 2), ("F", 2), ("pp",)], PK)
            yield
            for h in range(4):
                nm = (L, "xk", h)
                s_ = w_get(nm)
                bk, bkk = ps1()
                mmg(bk[:, 0:256], [(wsl[:, s_, k * 128:(k + 1) * 128], memn[:, k, :]) for k in range(8)], [("w", s_)] + PK, bkk)
                w_rel(nm)
                act(Ft[0][:, 0:256], bk[:, 0:256], AF.Copy, bkk, [("F", 0)])
                act(Bt[0][:, 0:256], bk[:, 0:256], AF.Square, bkk, [("B", 0)])
                yield
                b2, b2k = ps1()
                mmg(b2[:, 0:256], [(ones, Bt[0][:, 0:256])], [("B", 0)] + CK, b2k)
                act(Ft[1][:, 0:256], b2[:, 0:256], AF.Ln, b2k, [("F", 1)], bias=EPS, scale=1.0 / 128)
                act(Ft[1][:, 0:256], Ft[1][:, 0:256], AF.Exp, [("F", 1)], [("F", 1)], scale=-0.5)
                stt(kTx[:, h, :], Ft[0][:, 0:256], pp[:, PP_XKG:PP_XKG + 1], Ft[1][:, 0:256], ALU.mult, ALU.mult,
                    [("F", 0), ("F", 1), ("pp",)], [("kTx", h)])
                yield
            nxv = [(L, "xv", j) for j in range(4)]
            sxv = [w_get(n_) for n_ in nxv]
            for mc in range(2):
                bk, bkk = ps1()
                for j in range(4):
                    mmg(bk[:, j * 128:(j + 1) * 128],
                        [(memn[:, k, mc * 128:(mc + 1) * 128], wsl[:, sxv[j], k * 128:(k + 1) * 128]) for k in range(8)],
                        [("w", sxv[j])] + PK, bkk)
                act(Vx[:, mc, :], bk, AF.Copy, bkk, [("Vx", mc)])
                yield
            for n_ in nxv:
                w_rel(n_)
            nxq = [(L, "xq", h) for h in range(4)]
            sxq = [w_get(n_) for n_ in nxq]
            for tg in range(NTG):
                for h in range(4):
                    bk, bkk = ps1()
                    mmg(bk, [(wsl[:, sxq[h], k * 128:(k + 1) * 128], hT[:, k, tgs(tg)]) for k in range(8)],
                        [("w", sxq[h])] + hkeys(tg), bkk)
                    act(Ft[0][:, 0:512], bk, AF.Copy, bkk, [("F", 0)])
                    act(Bt[0][:], bk, AF.Square, bkk, [("B", 0)])
                    yield
                    b2, b2k = ps1()
                    mmg(b2, [(ones, Bt[0][:])], [("B", 0)] + CK, b2k)
                    act(Ft[1][:, 0:512], b2, AF.Ln, b2k, [("F", 1)], bias=EPS, scale=1.0 / 128)
                    act(Ft[1][:, 0:512], Ft[1][:, 0:512], AF.Exp, [("F", 1)], [("F", 1)], scale=-0.5)
                    stt(qT[:, h, :], Ft[0][:, 0:512], pp[:, PP_XQG:PP_XQG + 1], Ft[1][:, 0:512], ALU.mult, ALU.mult,
                        [("F", 0), ("F", 1), ("pp",)], [("qT", h)])
                    yield
                    PT = [Bt[1], Bt[2]]
                    for mc in range(2):
                        sc, sck = ps1()
                        mmg(sc, [(kTx[:, h, mc * 128:(mc + 1) * 128], qT[:, h, :])], [("kTx", h), ("qT", h)], sck)
                        act(PT[mc][:], sc, AF.Exp, sck, [("B", 1 + mc)], scale=float(128 ** -0.5))
                    yield
                    num, numk = ps1()
                    mmg(num, [(Vx[:, mc, h * 128:(h + 1) * 128], PT[mc][:]) for mc in range(2)],
                        [("Vx", 0), ("Vx", 1), ("B", 1), ("B", 2)], numk)
                    dn, dnk = ps1()
                    mmg(dn, [(ones, PT[mc][:]) for mc in range(2)], [("B", 1), ("B", 2)] + CK, dnk)
                    act(Ft[2][:, 0:512], dn, AF.Ln, dnk, [("F", 2)])
                    act(Ft[2][:, 0:512], Ft[2][:, 0:512], AF.Exp, [("F", 2)], [("F", 2)], scale=-1.0)
                    tt(oTb[:, h, tgs(tg)], num, Ft[2][:, 0:512], ALU.mult, numk + [("F", 2)] + og, [ok(ob, h, tg)])
                    yield
            for n_ in nxq:
                w_rel(n_)

        def lru_main(L, ob):
            oTb, og = OB[ob], OG[ob]
            nxb2 = [(L, "xb2", c) for c in range(4)]
            ngb = [(L, "gb", c) for c in range(4)]
            swr = w_get((L, "wrwi"))
            for c in range(4):
                sx = w_get(nxb2[c])
                sg = w_get(ngb[c])
                for tg in range(NTG):
                    xbt, xbp = Ft[tg % 2], Ft[(tg + 1) % 2]
                    xk, xpk = ("F", tg % 2), ("F", (tg + 1) % 2)
                    bx, bxk = ps1()
                    mmg(bx, [(wsl[:, sx, k * 128:(k + 1) * 128], hT[:, k, tgs(tg)]) for k in range(8)],
                        [("w", sx)] + hkeys(tg), bxk)
                    bg, bgk = ps1()
                    mmg(bg, [(wsl[:, sg, k * 128:(k + 1) * 128], hT[:, k, tgs(tg)]) for k in range(8)],
                        [("w", sg)] + hkeys(tg), bgk)
                    yield
                    if tg == 0:
                        vcopy(xbt[:, 0:3], smc(SM_HALO + c * 4, 3), [("halo",)], [xk])
                    else:
                        vcopy(xbt[:, 0:3], xbp[:, 512:515], [xpk], [xk])
                    act(xbt[:, 3:515], bx, AF.Copy, bxk, [xk])
                    xc = Ft[2]
                    ts(xc[:, 0:512], xbt[:, 0:512], pp[:, PP_CW + c:PP_CW + c + 1], pp[:, PP_CB + c:PP_CB + c + 1],
                       ALU.mult, ALU.add, [xk, ("pp",)], [("F", 2)])
                    for k in range(1, 4):
                        stt(xc[:, 0:512], xbt[:, k:k + 512], pp[:, PP_CW + k * 4 + c:PP_CW + k * 4 + c + 1], xc[:, 0:512],
                            ALU.mult, ALU.add, [xk, ("pp",), ("F", 2)], [("F", 2)])
                    act(Bt[0][:], xc[:, 0:512], AF.Copy, [("F", 2)], [("B", 0)])
                    br_, brk = ps1()
                    mmg(br_, [(wsl[:, swr, c * 128:(c + 1) * 128], Bt[0][:])], [("w", swr), ("B", 0)], brk)
                    bi_, bik = ps1()
                    mmg(bi_, [(wsl[:, swr, 512 + c * 128:512 + (c + 1) * 128], Bt[0][:])], [("w", swr), ("B", 0)], bik)
                    yield
                    fa, fb_, fm = Ft[3], Ft[4], Ft[5]
                    act(fa[:, 0:512], br_, AF.Tanh, brk + [("hb",)], [("F", 3)], bias=smc(SM_HB + 32 + c), scale=0.5)
                    act(fb_[:, 0:512], bi_, AF.Tanh, bik + [("hb",)], [("F", 4)], bias=smc(SM_HB + 36 + c), scale=0.5)
                    act(fa[:, 0:512], fa[:, 0:512], AF.Exp, [("F", 3), ("c1",)], [("F", 3)],
                        bias=smc(SM_C1 + c), scale=smc(SM_C1 + c))
                    tt(fm[:, 0:512], fa[:, 0:512], fa[:, 0:512], ALU.mult, [("F", 3)], [("F", 5)])
                    act(fm[:, 0:512], fm[:, 0:512], AF.Ln, [("F", 5)], [("F", 5)], bias=1.0, scale=-1.0)
                    act(fm[:, 0:512], fm[:, 0:512], AF.Exp, [("F", 5)], [("F", 5)], scale=0.5)
                    stt(fb_[:, 0:512], fb_[:, 0:512], 1.0, xc[:, 0:512], ALU.add, ALU.mult, [("F", 4), ("F", 2)], [("F", 4)])
                    stt(fb_[:, 0:512], fb_[:, 0:512], 0.5, fm[:, 0:512], ALU.mult, ALU.mult, [("F", 4), ("F", 5)], [("F", 4)])
                    hi = 0.0 if tg == 0 else smc(SM_HC + c)
                    pi = 1.0 if tg == 0 else smc(SM_PC + c)
                    tk.op("dve", [("F", 3), ("F", 4), ("hc", c)], [("F", 2)],
                          lambda e, hi=hi: e.tensor_tensor_scan(out=xc[:, 0:512], data0=fa[:, 0:512], data1=fb_[:, 0:512],
                                                                initial=hi, op0=ALU.mult, op1=ALU.add))
                    tk.op("dve", [("F", 3), ("zero",), ("pc", c)], [("F", 5)],
                          lambda e, pi=pi: e.tensor_tensor_scan(out=fm[:, 0:512], data0=fa[:, 0:512], data1=zero_bc,
                                                                initial=pi, op0=ALU.mult, op1=ALU.add))
                    vcopy(smc(SM_HC + c), xc[:, 511:512], [("F", 2)], [("hc", c)])
                    vcopy(smc(SM_PC + c), fm[:, 511:512], [("F", 5)], [("pc", c)])
                    act(Bt[1][:], bg, AF.Gelu_apprx_tanh, bgk, [("B", 1)])
                    tt(oTb[:, c, tgs(tg)], Bt[1][:], xc[:, 0:512], ALU.mult, [("B", 1), ("F", 2)] + og, [ok(ob, c, tg)])
                    tt(G2v[:, c, tgs(tg)], Bt[1][:], fm[:, 0:512], ALU.mult, [("B", 1), ("F", 5)] + GG, [("g2", c, tg)])
                    yield
                w_rel(nxb2[c])
                w_rel(ngb[c])
            w_rel((L, "wrwi"))


        def run(gen):
            for _ in gen:
                pass

        def load_params(L):
            spdma(pp[:], d_pp[L], [], [("pp",)])
            ts(smc(SM_HB, 40), pp[:, PP_BG:PP_BG + 40], 0.5, None, ALU.mult, None, [("pp",)], [("hb",)])
            act(smc(SM_TMP, 4), pp[:, PP_LAM:PP_LAM + 4], AF.Exp, [("pp",)], [("smtmp",)], scale=-1.0)
            act(smc(SM_TMP, 4), smc(SM_TMP, 4), AF.Ln, [("smtmp",)], [("smtmp",)], bias=1.0)
            ts(smc(SM_C1, 4), smc(SM_TMP, 4), -4.0, None, ALU.mult, None, [("smtmp",)], [("c1",)])
            act(smc(SM_ES, 8), pp[:, PP_SINK:PP_SINK + 8], AF.Exp, [("pp",)], [("es",)])

        for L in range(n_layers):
            if L == 0:
                load_params(0)
                rmsnorm_to_hT(PP_GMIX)
            if "h" in debug and L == 0:
                spdma(d_dbg["h"], hT[:], [("h", c, tg) for c in range(8) for tg in range(NTG)], [("dbg", len(tk.st))])
            for c in (3, 4, 5, 6, 7):
                spdma(d_xs[:, c - 3, :], xT[:, c, :], [], [("x", c, tg) for tg in range(NTG)] + [("xs", c), ("xg", c)])

            nxb = [(L, "xb", c) for c in range(4)]
            bk, bkk = ps1()
            for c in range(4):
                s = w_get(nxb[c])
                mmg(bk[:, c * 4:c * 4 + 3],
                    [(wsl[:, s, k * 128:(k + 1) * 128], hT[:, k, T - 3:T]) for k in range(8)],
                    [("w", s)] + hkeys(3), bkk)
                w_rel(nxb[c])
            tk.op("dve", [], [("pub1",)], lambda e: e.memset(smc(SM_PUB1, 16), 0.0))
            for c in range(4):
                act(smc(SM_PUB1 + c * 4, 3), bk[:, c * 4:c * 4 + 3], AF.Copy, bkk, [("pub1",)])
            spdma(d_src1, smc(SM_PUB1, 16), [("pub1",)], [("src1",)])
            ccn = f"cc{2 * L}"
            tk.dma("pool", ccn, [("src1",)], [("gat1",)],
                   lambda e: e.collective_compute("AllGather", ALU.bypass, replica_groups=[list(range(NCORE))],
                                                  ins=[d_src1], outs=[d_gat1]), inc=1)
            spdma(g1[:], d_gat1.rearrange("(r p) n -> p r n", p=128), [("gat1",)], [("g1",)])
            ts(smc(SM_HALO, 16), g1[:, 0, :], cf[:, 0:1], None, ALU.mult, None, [("g1",), ("cf",)], [("halo",)])
            for r in range(1, 8):
                stt(smc(SM_HALO, 16), g1[:, r, :], cf[:, r:r + 1], smc(SM_HALO, 16), ALU.mult, ALU.add,
                    [("g1",), ("cf",), ("halo",)], [("halo",)])

            sk = w_get((L, "k"))
            for tg in range(NTG):
                bk, bkk = ps1()
                mmg(bk, [(wsl[:, sk, k * 128:(k + 1) * 128], hT[:, k, tgs(tg)]) for k in range(8)],
                    [("w", sk)] + hkeys(tg), bkk)
                normrope(bk, bkk, PP_SKG, kT[:, 128 + tg * TG:128 + (tg + 1) * TG], [("kT", 1 + tg)], tg * TG)
            w_rel((L, "k"))
            sv = w_get((L, "v"))
            for tg in range(NTG):
                bk, bkk = ps1()
                for bi in range(4):
                    blk = tg * 4 + bi
                    mmg(bk[:, bi * 128:(bi + 1) * 128],
                        [(hT[:, k, blk * 128:(blk + 1) * 128], wsl[:, sv, k * 128:(k + 1) * 128]) for k in range(8)],
                        [("w", sv)] + hkeys(tg), bkk)
                act(va[:, 1 + tg * 4:5 + tg * 4, :, 0:64], bk.rearrange("p (b k d) -> p b k d", b=4, k=2),
                    AF.Copy, bkk, [("va", 1 + tg)])
            w_rel((L, "v"))

            run(gmlp_main(L, 0))
            if "o_gm" in debug and L == 0:
                spdma(d_dbg["o_gm"], OB[0], ALLO(0), [("dbg", len(tk.st))])
            interleave(lru_main(L, 1), merge_gen(L, 0, 0, True), 32.0 / 48)

            pub2 = Ft[0]
            act(pub2[:, 0:4], smc(SM_PC, 4), AF.Copy, [("pc", c) for c in range(4)], [("F", 0)])
            act(pub2[:, 4:8], smc(SM_HC, 4), AF.Copy, [("hc", c) for c in range(4)], [("F", 0)])
            act(pub2[:, 8:136], kT[:, T:T + 128], AF.Copy, [("kT", 4)], [("F", 0)])
            act(pub2[:, 136:264].rearrange("p (k d) -> p k d", k=2), va[:, 16, :, 0:64], AF.Copy, [("va", 4)], [("F", 0)])
            spdma(d_src2, pub2[:, 0:264], [("F", 0)], [("src2",)])
            ccn = f"cc{2 * L + 1}"
            tk.dma("pool", ccn, [("src2",)], [("gat2",)],
                   lambda e: e.collective_compute("AllGather", ALU.bypass, replica_groups=[list(range(NCORE))],
                                                  ins=[d_src2], outs=[d_gat2]), inc=1)
            spdma(g2[:], d_gat2.rearrange("(r p) n -> p r n", p=128)[:, :, 0:8], [("gat2",)], [("g2",)])
            hin = smc(SM_HINIT, 4)
            tmpa = smc(SM_TMP, 4)
            tmpb = smc(SM_TMP + 4, 4)
            tk.op("dve", [], [("hinit",)], lambda e: e.memset(hin, 0.0))
            for r in range(8):
                sel = cf[:, 8 + r:9 + r]
                ts(tmpa, g2[:, r, 0:4], -1.0, sel, ALU.add, ALU.mult, [("g2",), ("cf",)], [("smtmp",)])
                ts(tmpa, tmpa, 1.0, None, ALU.add, None, [("smtmp",)], [("smtmp",)])
                ts(tmpb, g2[:, r, 4:8], sel, None, ALU.mult, None, [("g2",), ("cf",)], [("smtmp2",)])
                tt(hin, hin, tmpa, ALU.mult, [("hinit",), ("smtmp",)], [("hinit",)])
                tt(hin, hin, tmpb, ALU.add, [("hinit",), ("smtmp2",)], [("hinit",)])
            acc = Ft[3]
            for r in range(8):
                tmp = Ft[1 + (r % 2)]
                spdma(tmp[:, 0:256], d_gat2[r * 128:(r + 1) * 128, 8:264], [("gat2",)], [("F", 1 + (r % 2))])
                if r == 0:
                    ts(acc[:, 0:256], tmp[:, 0:256], cf[:, 0:1], None, ALU.mult, None, [("F", 1), ("cf",)], [("F", 3)])
                else:
                    stt(acc[:, 0:256], tmp[:, 0:256], cf[:, r:r + 1], acc[:, 0:256], ALU.mult, ALU.add,
                        [("F", 1 + (r % 2)), ("cf",), ("F", 3)], [("F", 3)])
            act(kT[:, 0:128], acc[:, 0:128], AF.Copy, [("F", 3)], [("kT", 0)])
            act(va[:, 0, :, 0:64], acc[:, 128:256].rearrange("p (k d) -> p k d", k=2), AF.Copy, [("F", 3)], [("va", 0)])
            for c in range(4):
                stt(OB[1][:, c, :], G2v[:, c, :], smc(SM_HINIT + c), OB[1][:, c, :], ALU.mult, ALU.add,
                    [("hinit",)] + [("g2", c, tg) for tg in range(NTG)] + [ok(1, c, tg) for tg in range(NTG)] + GG + OG[1],
                    [ok(1, c, tg) for tg in range(NTG)])
            if "o_lru" in debug and L == 0:
                spdma(d_dbg["o_lru"], OB[1], ALLO(1) + OG[1], [("dbg", len(tk.st))])

            interleave(swa_main(L, 0), merge_gen(L, 1, 1, False), 32.0 / 100)
            if "o_swa" in debug and L == 0:
                spdma(d_dbg["o_swa"], OB[0], ALLO(0), [("dbg", len(tk.st))])
            interleave(xattn_main(L, 1), merge_gen(L, 2, 0, False), 32.0 / 76)
            if "o_xa" in debug and L == 0:
                spdma(d_dbg["o_xa"], OB[1], ALLO(1) + OG[1], [("dbg", len(tk.st))])
            run(merge_gen(L, 3, 1, False))
            for c in (3, 4, 5, 6, 7):
                spdma(xT[:, c, :], d_xs[:, c - 3, :], [("xs", c)], [("x", c, tg) for tg in range(NTG)] + [("xg", c)])

            for m in range(8):
                nm = (L, "wo", m)
                s = w_get(nm)
                for tg in range(NTG):
                    bk, bkk = ps1()
                    mmg(bk, [(wsl[:, s, k * 128:(k + 1) * 128], mg[:, k, tgs(tg)]) for k in range(8)],
                        [("w", s)] + [("mg", k, tg) for k in range(8)], bkk)
                    stt(xT[:, m, tgs(tg)], bk, 0.5, xT[:, m, tgs(tg)], ALU.mult, ALU.add, bkk + [("x", m, tg)], [("x", m, tg)])
                w_rel(nm)
            if "xmix" in debug and L == 0:
                spdma(d_dbg["xmix"], xT[:], [("x", c, tg) for c in range(8) for tg in range(NTG)], [("dbg", len(tk.st))])

            rmsnorm_to_hT(PP_GMLP)
            if L + 1 < n_layers:
                load_params(L + 1)
            hds = [(Pt[:].rearrange("p (f t) -> p f t", f=4), [("P", 0), ("P", 1)]),
                   (qT, [("qT", g) for g in range(4)])]
            for fb in range(8):
                n1 = [(L, "f1", fb, fc) for fc in range(4)]
                n2 = [(L, "f2", fb, mp) for mp in range(4)]
                s1 = [w_get(n_) for n_ in n1]

                def ff1(tg):
                    hd, hdk = hds[tg % 2]
                    for fc in range(4):
                        bk, bkk = ps1()
                        mmg(bk, [(wsl[:, s1[fc], k * 128:(k + 1) * 128], hT[:, k, tgs(tg)]) for k in range(8)],
                            [("w", s1[fc])] + hkeys(tg), bkk)
                        rl = Bt[fc % 2]
                        act(rl[:], bk, AF.Relu, bkk, [("B", fc % 2)])
                        stt(hd[:, fc, :], bk, 0.0, rl[:], ALU.max, ALU.mult, bkk + [("B", fc % 2)], hdk)

                ff1(0)
                s2 = [w_get(n_) for n_ in n2]
                for tg in range(NTG):
                    if tg + 1 < NTG:
                        ff1(tg + 1)
                    else:
                        for n_ in n1:
                            w_rel(n_)
                    hd, hdk = hds[tg % 2]
                    for m in range(8):
                        sl = s2[m // 2]
                        mm = m % 2
                        bk, bkk = ps1()
                        mmg(bk, [(wsl[:, sl, fc * 256 + mm * 128:fc * 256 + mm * 128 + 128], hd[:, fc, :]) for fc in range(4)],
                            [("w", sl)] + hdk, bkk)
                        tt(xT[:, m, tgs(tg)], bk, xT[:, m, tgs(tg)], ALU.add, bkk + [("x", m, tg)], [("x", m, tg)])
                    if fb == 7 and L + 1 < n_layers:
                        norm_tg(PP_GMIX, tg)
                for n_ in n2:
                    w_rel(n_)

        for c in range(8):
            spdma(d_y[:, c, :], xT[:, c, :], [("x", c, tg) for tg in range(NTG)], [("y", c)])
        need = {}
        for c in range(8):
            n, v = tk.st[("y", c)][0]
            need[n] = max(need.get(n, 0), v)
        for k in list(tk.st):
            if k[0] == "dbg":
                n, v = tk.st[k][0]
                need[n] = max(need.get(n, 0), v)
        for n, v in need.items():
            nc.sync.wait_ge(sems[n], v)
    return nc, worder


def _tile_k1024(Wcols):
    return np.ascontiguousarray(Wcols.reshape(8, 128, 128).transpose(1, 0, 2).reshape(128, 1024))


def _tile_k512(Wcols):
    return np.ascontiguousarray(Wcols.reshape(4, 128, 256).transpose(1, 0, 2).reshape(128, 1024))


def _layer_tiles(l, I):
    w_in = I["w_in"][l]
    tl = {}
    OFF_GM, OFF_LRU, OFF_Q, OFF_K, OFF_V, OFF_XQ, OFF_G = 0, 1024, 2048, 2560, 2688, 2816, 3328
    tl[(l, "k")] = _tile_k1024(w_in[:, OFF_K:OFF_K + 128])
    tl[(l, "v")] = _tile_k1024(w_in[:, OFF_V:OFF_V + 128])
    for c in range(4):
        tl[(l, "xb", c)] = _tile_k1024(w_in[:, OFF_LRU + c * 128:OFF_LRU + (c + 1) * 128])
        tl[(l, "xb2", c)] = tl[(l, "xb", c)]
        tl[(l, "gb", c)] = _tile_k1024(w_in[:, OFF_LRU + 512 + c * 128:OFF_LRU + 512 + (c + 1) * 128])
    wrwi = np.zeros((128, 1024), np.float32)
    for c in range(4):
        for hb in range(2):
            blk = 2 * c + hb
            wrwi[hb * 64:(hb + 1) * 64, c * 128 + hb * 64:c * 128 + (hb + 1) * 64] = I["lru_wr"][l][blk]
            wrwi[hb * 64:(hb + 1) * 64, 512 + c * 128 + hb * 64:512 + c * 128 + (hb + 1) * 64] = I["lru_wi"][l][blk]
    tl[(l, "wrwi")] = wrwi
    for g in range(4):
        cols = np.concatenate([w_in[:, OFF_Q + g * 64:OFF_Q + (g + 1) * 64],
                               w_in[:, OFF_Q + (g + 4) * 64:OFF_Q + (g + 5) * 64]], axis=1)
        tl[(l, "q", g)] = _tile_k1024(cols)
    for b in range(4):
        for mp in range(4):
            tl[(l, "mb", b, mp)] = _tile_k512(I["w_branch"][l][b][:, mp * 256:(mp + 1) * 256])
        for m in range(8):
            tl[(l, "mgt", b, m)] = _tile_k1024(w_in[:, OFF_G + b * 1024 + m * 128:OFF_G + b * 1024 + (m + 1) * 128])
    for c in range(4):
        tl[(l, "gu", c)] = _tile_k1024(w_in[:, OFF_GM + c * 128:OFF_GM + (c + 1) * 128])
        tl[(l, "gv", c)] = _tile_k1024(w_in[:, OFF_GM + 512 + c * 128:OFF_GM + 512 + (c + 1) * 128])
    wst = np.zeros((128, 1024), np.float32)
    for g in range(4):
        wst[:, g * 128:(g + 1) * 128] = I["gm_ws"][l][g].T
    tl[(l, "gws")] = wst
    wkv = I["w_mem_kv"][l]
    for h in range(4):
        tl[(l, "xk", h)] = _tile_k1024(wkv[:, h * 128:(h + 1) * 128])
        tl[(l, "xv", h)] = _tile_k1024(wkv[:, 512 + h * 128:512 + (h + 1) * 128])
        tl[(l, "xq", h)] = _tile_k1024(w_in[:, OFF_XQ + h * 128:OFF_XQ + (h + 1) * 128])
    for m in range(8):
        tl[(l, "wo", m)] = _tile_k1024(I["w_out"][l][:, m * 128:(m + 1) * 128])
    w1, w2 = I["w_ff1"][l], I["w_ff2"][l]
    for fb in range(8):
        for fc in range(4):
            f0 = fb * 512 + fc * 128
            tl[(l, "f1", fb, fc)] = _tile_k1024(w1[:, f0:f0 + 128])
        for mp in range(4):
            tl[(l, "f2", fb, mp)] = _tile_k512(w2[fb * 512:(fb + 1) * 512, mp * 256:(mp + 1) * 256])
    assert len(tl) == TPL, len(tl)
    return tl


def _layer_pp(l, I):
    pp = np.zeros((128, PP_N), np.float32)
    fm = lambda v, n: np.asarray(v, np.float32).reshape(n, 128).T
    pp[:, PP_GMIX:PP_GMIX + 8] = fm(I["norm_mix"][l], 8)
    pp[:, PP_GMLP:PP_GMLP + 8] = fm(I["norm_mlp"][l], 8)
    pp[:, PP_GMEM:PP_GMEM + 8] = fm(I["norm_mem"][l], 8)
    for b in range(4):
        pp[:, PP_BG + b * 8:PP_BG + (b + 1) * 8] = fm(I["b_gate"][l][b], 8)
    pp[:, PP_BR:PP_BR + 4] = fm(I["lru_br"][l], 4)
    pp[:, PP_BI:PP_BI + 4] = fm(I["lru_bi"][l], 4)
    for k in range(4):
        pp[:, PP_CW + k * 4:PP_CW + (k + 1) * 4] = fm(I["lru_conv_w"][l][k], 4)
    pp[:, PP_CB:PP_CB + 4] = fm(I["lru_conv_b"][l], 4)
    pp[:, PP_LAM:PP_LAM + 4] = fm(I["lru_lambda"][l], 4)
    pp[:, PP_SQG] = np.tile(I["swa_q_gain"][l], 2)
    pp[:, PP_SKG] = np.tile(I["swa_k_gain"][l], 2)
    pp[:, PP_XQG] = I["xa_q_gain"][l]
    pp[:, PP_XKG] = I["xa_k_gain"][l]
    pp[:, PP_SINK:PP_SINK + 8] = I["swa_sinks"][l][None, :]
    pp[:, PP_VG:PP_VG + 512] = I["gm_v_gain"][l][None, :]
    pp[:, PP_BS:PP_BS + 512] = I["gm_bs"][l].reshape(1, 512)
    return pp


def _consts(seg):
    cbm = np.zeros((128, CB_N), np.float32)
    pos = (seg * T + np.arange(T)).astype(np.float32)
    inv = (np.float32(500000.0) ** (-(np.arange(0, 16, 2, dtype=np.float32)) / np.float32(16))).astype(np.float32)
    ang = pos[None, :] * inv[:, None]
    cosv, sinv = np.cos(ang).astype(np.float32), np.sin(ang).astype(np.float32)
    cos_t = np.ones((128, T), np.float32)
    sin_t = np.zeros((128, T), np.float32)
    for hb in range(2):
        for d in range(16):
            cos_t[hb * 64 + d] = cosv[d % 8]
            sin_t[hb * 64 + d] = sinv[d % 8]
    cbm[:, CB_COS:CB_COS + T] = cos_t
    cbm[:, CB_SIN:CB_SIN + T] = sin_t
    j = np.arange(128)[:, None]
    i = np.arange(128)[None, :]
    prev = (j > i).astype(np.float32)
    cur = (j <= i).astype(np.float32)
    cbm[:, CB_MASKN:CB_MASKN + 128] = prev
    cbm[:, CB_MASKN + 128:CB_MASKN + 256] = cur
    cbm[:, CB_MASK0:CB_MASK0 + 128] = prev * (1.0 if seg > 0 else 0.0)
    cbm[:, CB_MASK0 + 128:CB_MASK0 + 256] = cur
    cbm[:, CB_ID:CB_ID + 128] = np.eye(128, dtype=np.float32)
    cbm[:, CB_ONES:CB_ONES + 128] = 1.0
    bo = np.zeros((128, 128), np.float32)
    bo[:64, :64] = 1.0
    bo[64:, 64:] = 1.0
    cbm[:, CB_BONES:CB_BONES + 128] = bo
    rot = np.zeros((128, 128), np.float32)
    for hb in range(2):
        for d in range(8):
            rot[hb * 64 + d + 8, hb * 64 + d] = -1.0
            rot[hb * 64 + d, hb * 64 + d + 8] = 1.0
    cbm[:, CB_ROT:CB_ROT + 128] = rot
    return cbm


def prep_inputs(I, n_layers, order):
    tl = {}
    pps = []
    for l in range(n_layers):
        tl.update(_layer_tiles(l, I))
        pps.append(_layer_pp(l, I))
    assert len(order) == len(tl) == n_layers * TPL, (len(order), len(tl))
    wts = np.ascontiguousarray(np.stack([tl[n] for n in order], 0))
    ppa = np.ascontiguousarray(np.stack(pps, 0))
    in_maps = []
    for core in range(NCORE):
        b, seg = core // 4, core % 4
        xs = I["x"][b, seg * T:(seg + 1) * T, :]
        xTc = np.ascontiguousarray(xs.T.reshape(8, 128, T).transpose(1, 0, 2))
        memT = np.ascontiguousarray(I["mem"][b].T.reshape(8, 128, 256).transpose(1, 0, 2))
        cfm = np.zeros((128, 16), np.float32)
        if seg > 0:
            cfm[:, core - 1] = 1.0
            for r in range(b * 4, core):
                cfm[:, 8 + r] = 1.0
        in_maps.append({"xT": xTc, "memT": memT, "wts": wts, "pp": ppa, "cb": _consts(seg), "cf": cfm})
    return in_maps


def assemble(res, key="yT"):
    out = np.zeros((2, 4 * T, 1024), np.float32)
    for core in range(NCORE):
        b, seg = core // 4, core % 4
        y = np.asarray(res[core][key]).astype(np.float32)
        out[b, seg * T:(seg + 1) * T, :] = y.transpose(2, 1, 0).reshape(T, 1024)
    return out


_NC_CACHE = {}


def kernel(**inputs):
    I = {k: np.asarray(v) for k, v in inputs.items()}
    n_layers = 4
    if n_layers not in _NC_CACHE:
        _, order = build(n_layers)
        _NC_CACHE[n_layers] = build(n_layers, order=order)
    nc, order = _NC_CACHE[n_layers]
    in_maps = prep_inputs(I, n_layers, order)
    res = run_bass_kernel_spmd(nc, in_maps, core_ids=list(range(NCORE)))
    return assemble(res.results)
```
